# Optimizing a Trainium2 kernel written in Bass

```python
import jax, jax.numpy as jnp
from jax import lax
import numpy as np

D_MODEL = 1024
BATCH = 16
SEQ = 2048
DEPTH = 2

CHUNK = 128
E_A = 2 * D_MODEL
N_GROUPS_A = 16
GROUP_A = E_A // N_GROUPS_A
E_B = D_MODEL
CONV_W = 3
D_FF = ((8 * D_MODEL // 3 + 255) // 256) * 256
N_A = (DEPTH + 1) // 2
N_B = DEPTH // 2
EPS = 1e-6

kernel_name = "hybrid_sgu_shortconv_swiglu"


def rms_norm(x, g):
    xf = x.astype(jnp.float32)
    y = xf * lax.rsqrt(jnp.mean(xf * xf, axis=-1, keepdims=True) + EPS)
    return (y * g.astype(jnp.float32)).astype(x.dtype)


def layer_norm(x, g, b):
    xf = x.astype(jnp.float32)
    mu = jnp.mean(xf, axis=-1, keepdims=True)
    xc = xf - mu
    y = xc * lax.rsqrt(jnp.mean(xc * xc, axis=-1, keepdims=True) + EPS)
    return (y * g.astype(jnp.float32) + b.astype(jnp.float32)).astype(x.dtype)


def spatial_gating_mixer(h, w_in, v_gain, v_bias, w_s, b_s, w_out):
    bsz, seq, _ = h.shape
    z = jax.nn.gelu(jnp.einsum('bsd,de->bse', h, w_in))
    u, v = jnp.split(z, 2, axis=-1)
    v = layer_norm(v, v_gain, v_bias)
    v = v.reshape(bsz, seq // CHUNK, CHUNK, N_GROUPS_A, GROUP_A)
    causal = jnp.tril(jnp.ones((CHUNK, CHUNK), dtype=bool))
    w = jnp.where(causal[None], w_s, jnp.zeros((), w_s.dtype))
    sv = jnp.einsum('hts,bnshc->bnthc', w, v) + b_s.T[None, None, :, :, None]
    y = u * sv.reshape(bsz, seq, E_A)
    return jnp.einsum('bse,ed->bsd', y, w_out)


def short_conv_mixer(h, w_in, conv_w, w_out):
    seq = h.shape[1]
    p = jnp.einsum('bsd,de->bse', h, w_in)
    b_gate, c_gate, hx = jnp.split(p, 3, axis=-1)
    z = c_gate * hx
    zp = jnp.pad(z, ((0, 0), (CONV_W - 1, 0), (0, 0)))
    conv = zp[:, 0:seq] * conv_w[0]
    for k in range(1, CONV_W):
        conv = conv + zp[:, k:k + seq] * conv_w[k]
    y = b_gate * conv
    return jnp.einsum('bse,ed->bsd', y, w_out)


def swiglu(h, w_gate, w_up, w_down):
    g = jnp.einsum('bsd,df->bsf', h, w_gate)
    u = jnp.einsum('bsd,df->bsf', h, w_up)
    return jnp.einsum('bsf,fd->bsd', jax.nn.silu(g) * u, w_down)


def setup_inputs(seed: int = 0) -> dict:
    key = jax.random.key(seed)
    ks = jax.random.split(key, 20)
    f32 = jnp.float32

    def nrm(k, shape, fan_in):
        return jax.random.normal(k, shape, f32) * (fan_in ** -0.5)

    def gain(k, shape):
        return 1.0 + 0.02 * jax.random.normal(k, shape, f32)

    return {
        "x": jax.random.normal(ks[0], (BATCH, SEQ, D_MODEL), f32),
        "mix_norm": gain(ks[1], (DEPTH, D_MODEL)),
        "ffn_norm": gain(ks[2], (DEPTH, D_MODEL)),
        "a_w_in": nrm(ks[3], (N_A, D_MODEL, 2 * E_A), D_MODEL),
        "a_v_gain": gain(ks[4], (N_A, E_A)),
        "a_v_bias": 0.02 * jax.random.normal(ks[5], (N_A, E_A), f32),
        "a_w_s": nrm(ks[6], (N_A, N_GROUPS_A, CHUNK, CHUNK), CHUNK),
        "a_b_s": gain(ks[7], (N_A, N_GROUPS_A, CHUNK)),
        "a_w_out": nrm(ks[8], (N_A, E_A, D_MODEL), E_A),
        "b_w_in": nrm(ks[9], (N_B, D_MODEL, 3 * E_B), D_MODEL),
        "b_conv_w": nrm(ks[10], (N_B, CONV_W, E_B), CONV_W),
        "b_w_out": nrm(ks[11], (N_B, E_B, D_MODEL), E_B),
        "ffn_w_gate": nrm(ks[12], (DEPTH, D_MODEL, D_FF), D_MODEL),
        "ffn_w_up": nrm(ks[13], (DEPTH, D_MODEL, D_FF), D_MODEL),
        "ffn_w_down": nrm(ks[14], (DEPTH, D_FF, D_MODEL), D_FF),
        "final_norm": gain(ks[15], (D_MODEL,)),
    }


def reference(x, mix_norm, ffn_norm, a_w_in, a_v_gain, a_v_bias, a_w_s, a_b_s, a_w_out,
              b_w_in, b_conv_w, b_w_out, ffn_w_gate, ffn_w_up, ffn_w_down, final_norm):
    for i in range(DEPTH):
        h = rms_norm(x, mix_norm[i])
        j = i // 2
        if i % 2 == 0:
            mix = spatial_gating_mixer(h, a_w_in[j], a_v_gain[j], a_v_bias[j],
                                       a_w_s[j], a_b_s[j], a_w_out[j])
        else:
            mix = short_conv_mixer(h, b_w_in[j], b_conv_w[j], b_w_out[j])
        x = x + mix
        x = x + swiglu(rms_norm(x, ffn_norm[i]), ffn_w_gate[i], ffn_w_up[i], ffn_w_down[i])
    return rms_norm(x, final_norm)
```

```python
import contextlib
import numpy as np
import concourse.bass as bass
import concourse.mybir as mybir
from concourse.bass_utils import run_bass_kernel_spmd

F32 = mybir.dt.float32
BF16 = mybir.dt.bfloat16
AF = mybir.ActivationFunctionType
ALU = mybir.AluOpType
AX = mybir.AxisListType

D = 1024
S = 2048
NCORES = 8
SEQ_PER_CORE = 2
T = 1024
NTILES = SEQ_PER_CORE * S // T
EA = 2048
DFF = 2816
KF = DFF // 128
EPS = 1e-6
NSLOT = 5
NXIN = 4
NIF = 10
SLOT_ELEMS = 4096

GC_MIX = 0
GC_FFN = 16
GC_VG = 32
GC_CW = 48
GC_FIN = 72
GC_VB = 80
GC_N = 96

ENGS = ("pe", "act", "dve", "pool", "sp")


class Tr:
    __slots__ = ("w", "r")

    def __init__(self):
        self.w = None
        self.r = {}


class Prog:
    def __init__(self, nc, stack):
        self.nc = nc
        self.stack = stack
        self.streams = {e: [] for e in ENGS}
        self.sems = {}
        self.count = {}
        self.seen = {e: {} for e in ENGS}
        for e in ENGS:
            self.new_sem(e)
        self.bank_i = 0

    def new_sem(self, key):
        self.sems[key] = self.stack.enter_context(self.nc.semaphore("s_" + key))
        self.count[key] = 0

    def _deps(self, eng, reads, writes, deps):
        need = {}

        def add(t, raw):
            if t is None:
                return
            k, v = t
            if k == eng and (eng == "pe" or not raw):
                return
            if v > need.get(k, 0):
                need[k] = v

        for tr in reads:
            add(tr.w, True)
        for tr in writes:
            add(tr.w, False)
            for k, v in tr.r.items():
                add((k, v), False)
        for t in deps:
            add(t, True)
        out = []
        seen = self.seen[eng]
        for k, v in need.items():
            if v > seen.get(k, 0):
                seen[k] = v
                out.append((k, v))
        return out

    def _update(self, ticket, reads, writes):
        k, v = ticket
        for tr in reads:
            if v > tr.r.get(k, 0):
                tr.r[k] = v
        for tr in writes:
            tr.w = ticket
            tr.r = {}

    def emit(self, eng, fns, reads=(), writes=(), deps=()):
        waits = self._deps(eng, reads, writes, deps)
        self.count[eng] += 1
        ticket = (eng, self.count[eng])
        sems = self.sems

        def run(h, waits=waits, fns=fns, eng=eng):
            for k, v in waits[1:]:
                h.wait_ge(sems[k], v)
            n = len(fns)
            for i, fn in enumerate(fns):
                ins = fn(h)
                if i == 0 and waits:
                    ins._wait_ge(sems[waits[0][0]], waits[0][1])
                if i == n - 1:
                    ins.then_inc(sems[eng], 1)

        self.streams[eng].append(run)
        self._update(ticket, reads, writes)
        return ticket

    def dma(self, eng, fn, sem, reads=(), writes=(), deps=()):
        waits = self._deps(eng, reads, writes, deps)
        if sem not in self.sems:
            self.new_sem(sem)
        self.count[sem] += 16
        ticket = (sem, self.count[sem])
        sems = self.sems

        def run(h, waits=waits, fn=fn, sem=sem):
            for k, v in waits:
                h.wait_ge(sems[k], v)
            fn(h).then_inc(sems[sem], 16)

        self.streams[eng].append(run)
        self._update(ticket, reads, writes)
        return ticket

    def wait(self, eng, tickets):
        waits = self._deps(eng, (), (), tickets)
        sems = self.sems

        def run(h, waits=waits):
            for k, v in waits:
                h.wait_ge(sems[k], v)

        self.streams[eng].append(run)

    def next_bank(self):
        b = self.bank_i % 8
        self.bank_i += 1
        return b


def build(cfg=None):
    cfg = dict(cfg or {})
    ntiles = cfg.get("ntiles", NTILES)
    layers = cfg.get("layers", (0, 1))
    do_mixer = cfg.get("mixer", True)
    do_ffn = cfg.get("ffn", True)
    do_final = cfg.get("final_norm", True)

    nc = bass.Bass("TRN2", target_bir_lowering=False)
    stack = contextlib.ExitStack()

    def din(name, shape, dt=F32):
        return nc.dram_tensor(name, list(shape), dt, kind="ExternalInput").ap()

    x = din("x", [SEQ_PER_CORE, S, D])
    y = nc.dram_tensor("y", [SEQ_PER_CORE, S, D], F32, kind="ExternalOutput").ap()
    wsrc = {
        "a_in": din("a_w_in", [D, 2 * EA]),
        "a_out": din("a_w_out", [EA, D]),
        "g0": din("ffn_w_gate0", [D, DFF]),
        "u0": din("ffn_w_up0", [D, DFF]),
        "d0": din("ffn_w_down0", [DFF, D]),
        "b_in": din("b_w_in", [D, 3 * D]),
        "b_out": din("b_w_out", [D, D]),
        "g1": din("ffn_w_gate1", [D, DFF]),
        "u1": din("ffn_w_up1", [D, DFF]),
        "d1": din("ffn_w_down1", [DFF, D]),
    }
    worder = ["a_in", "a_out", "g0", "u0", "d0", "b_in", "b_out", "g1", "u1", "d1"]
    wscr = {}
    for k in worder:
        K, E = wsrc[k].shape
        wscr[k] = nc.dram_tensor("scr_" + k, [K, E], BF16, kind="Internal").ap()
    ident_d = din("ident", [128, 128])
    mask_d = din("maskT", [128, 128])
    gcols_d = din("gcols", [128, GC_N])
    bsb_d = din("a_b_s_bc", [128, EA])
    wsT_d = din("a_w_sT", [128, 16 * 128])

    def sb(name, shape, dt):
        return stack.enter_context(nc.sbuf_tensor("sb_" + name, list(shape), dt))

    xT = sb("xT", [128, 8, T], F32)
    hT = sb("hT", [128, 8, T], BF16)
    sq = sb("sq", [128, 8, 512], BF16)
    rtmp = sb("rtmp", [128, 512], F32)
    rstd = [sb("rstd%d" % i, [128, 512], F32) for i in range(2)]
    ring = sb("ring", [128, NSLOT, SLOT_ELEMS], BF16)
    xin = [sb("xin%d" % i, [128, D], F32) for i in range(NXIN)]
    ident = sb("ident", [128, 128], F32)
    maskT = sb("maskT", [128, 128], F32)
    onesD = sb("onesD", [128, 128], BF16)
    ones16 = sb("ones16", [128, 128], BF16)
    epsc = sb("epsc", [128, 1], F32)
    gcols = sb("gcols", [128, GC_N], F32)
    wtril = sb("wtril", [128, 16, 128], BF16)
    Bfull = sb("Bfull", [128, 16, 128], F32)
    ytmp = [sb("ytmp%d" % i, [128, 512], F32) for i in range(2)]
    vsum = [sb("vsum%d" % i, [128, 4], F32) for i in range(2)]
    vst = [sb("vst%d" % i, [128, 8], F32) for i in range(2)]
    fst = [sb("fst%d" % i, [128, 8], F32) for i in range(2)]
    zhalo = sb("zhalo", [128, 8, 2], F32)
    SH = 35200
    shared = sb("shared", [128, SH], BF16)

    def shv(off, n, dt):
        ap = shared[:, off:off + n]
        return ap.bitcast(F32) if dt == F32 else ap

    uT = shv(0, 16 * T, BF16).rearrange("p (m t) -> p m t", m=16)
    vsb = [shv(16384 + i * 4096, 4096, F32) for i in range(2)]
    vhat = [shv(24576 + i * 2048, 2048, BF16) for i in range(2)]
    aT = shv(0, KF * T, BF16).rearrange("p (m t) -> p m t", m=KF)
    sgbuf = [shv(22528 + i * 1024, 1024, F32) for i in range(3)]
    ost = [shv(25600 + i * 2048, 2048, F32) for i in range(2)]
    ZW = T + 4
    zT = shv(0, 8 * ZW * 2, F32).rearrange("p (m t) -> p m t", m=8)
    z_end = 8 * ZW * 2
    cvbuf = [shv(z_end + i * 2048, 2048, F32) for i in range(4)]
    y1_off = z_end + 4 * 2048
    y1T = shv(y1_off, 8 * T, BF16).rearrange("p (m t) -> p m t", m=8)
    csb = [shv(y1_off + 8 * T + i * 1024, 1024, F32) for i in range(2)]
    assert y1_off + 8 * T + 2048 <= SH
    bsb = shv(16384, 4096, F32).rearrange("p (h t) -> p h t", h=16)
    wst = shv(24576, 4096, F32).rearrange("p (h t) -> p h t", h=16)

    banks = [stack.enter_context(nc.psum_tensor("ps%d" % i, [128, 512], F32)) for i in range(8)]

    P = Prog(nc, stack)
    tr_bank = [Tr() for _ in range(8)]
    tr_xT = [[Tr() for _ in range(2)] for _ in range(8)]
    tr_hT = [[Tr() for _ in range(2)] for _ in range(8)]
    tr_sq = [Tr() for _ in range(8)]
    tr_rtmp = Tr()
    tr_rscr = Tr()
    tr_rstd = [Tr(), Tr()]
    tr_slot = [Tr() for _ in range(NSLOT)]
    tr_xin = [Tr() for _ in range(NXIN)]
    tr_const = Tr()
    tr_ident, tr_mask, tr_gcols, tr_k, tr_mixa = Tr(), Tr(), Tr(), Tr(), Tr()
    tr_ytmp = [Tr(), Tr()]
    tr_u = [[Tr() for _ in range(2)] for _ in range(16)]
    tr_vsb = [Tr(), Tr()]
    tr_vhat = [Tr(), Tr()]
    tr_vst = [[Tr() for _ in range(8)] for _ in range(2)]
    tr_a = [[Tr() for _ in range(2)] for _ in range(KF)]
    tr_sg = [Tr() for _ in range(3)]
    tr_ost = [Tr(), Tr()]
    tr_z = [Tr() for _ in range(8)]
    tr_cv = [Tr() for _ in range(4)]
    tr_y1 = [[Tr() for _ in range(2)] for _ in range(8)]
    tr_csb = [Tr(), Tr()]
    tr_fst = [[Tr() for _ in range(8)] for _ in range(2)]
    tr_stg = Tr()
    tr_zh = Tr()
    state = {"slot": 0, "sg": 0, "cs": 0, "fin": 0, "yt": 0}

    def HS(h):
        return slice(h * 512, (h + 1) * 512)

    stg_trs = []

    def stg_new():
        t = Tr()
        stg_trs.append(t)
        return t

    def ld(dst, src, sem, t):
        return P.dma("sp", lambda h: h.dma_start(out=dst, in_=src), sem, writes=[t])

    use_a = do_mixer and 0 in layers
    ld(ident[:], ident_d, "c0", tr_ident)
    ld(gcols[:], gcols_d, "c2", tr_gcols)
    P.emit("dve", [lambda h: h.memset(onesD[:], 1.0 / D)], writes=[tr_k])
    P.emit("dve", [lambda h: h.memset(epsc[:], EPS)], writes=[tr_k])

    def setup_late_dmas():
        if use_a:
            ld(maskT[:], mask_d, "c1", tr_mask)
            P.dma("sp", lambda h: h.dma_start(out=wst, in_=wsT_d.rearrange("p (h t) -> p h t", h=16)), "c4", writes=[stg_new()])
            P.dma("sp", lambda h: h.dma_start(out=bsb, in_=bsb_d.rearrange("p (h t) -> p h t", h=16)), "c5", writes=[stg_new()])

    def setup_mixa():
        for hd in range(16):
            P.emit("dve", [lambda h, hd=hd: h.tensor_tensor(out=wtril[:, hd, :], in0=wst[:, hd, :], in1=maskT[:], op=ALU.mult)],
                   reads=stg_trs + [tr_stg, tr_mask], writes=[tr_mixa])
        P.emit("dve", [lambda h: h.memset(ones16[:], 1.0)], writes=[tr_mixa])
        for q in range(4):
            bk = P.next_bank()
            P.emit("pe", [lambda h, q=q, bk=bk: h.matmul(banks[bk][:], lhsT=ones16[:],
                                                         rhs=wtril[:].rearrange("p h t -> p (h t)")[:, q * 512:(q + 1) * 512], start=True, stop=True)],
                   reads=[tr_mixa], writes=[tr_bank[bk]])
            for i in range(4):
                hd = 4 * q + i
                P.emit("dve", [lambda h, i=i, hd=hd, bk=bk: h.scalar_tensor_tensor(
                    out=Bfull[:, hd, :], in0=banks[bk][:, i * 128:(i + 1) * 128], scalar=gcols[:, GC_VB + hd:GC_VB + hd + 1],
                    in1=bsb[:, hd, :], op0=ALU.mult, op1=ALU.add)],
                    reads=stg_trs + [tr_bank[bk], tr_gcols], writes=[tr_mixa, tr_stg])

    cv_ticket = {}
    pp_state = {"i": 0}
    pp_last = {}

    def pp_dma(fn):
        j = pp_state["i"] % NIF
        pp_state["i"] += 1
        sem = "pp%d" % j
        deps = [pp_last[sem]] if sem in pp_last else []
        pp_last[sem] = P.dma("pool", fn, sem, deps=deps)

    used = set()
    if do_mixer and 0 in layers:
        used |= {"a_in", "a_out"}
    if do_mixer and 1 in layers:
        used |= {"b_in", "b_out"}
    if do_ffn:
        for l in layers:
            used |= {"g%d" % l, "u%d" % l, "d%d" % l}
    def cwb(k):
        return 256 if (k == "a_out" or k[0] == "d") else 512

    for k in worder:
        if k not in used:
            continue
        K, E = wsrc[k].shape
        for j in range((E + cwb(k) - 1) // cwb(k)):
            c0, c1 = j * cwb(k), min(E, (j + 1) * cwb(k))
            for kc in range(K // 128):
                pp_dma(lambda h, k=k, kc=kc, c0=c0, c1=c1: h.dma_start(out=wscr[k][kc * 128:(kc + 1) * 128, c0:c1],
                                                                       in_=wsrc[k][kc * 128:(kc + 1) * 128, c0:c1]))
            cv_ticket[(k, j)] = list(pp_last.values())

    def scr3(k):
        return wscr[k].rearrange("(k p) e -> p k e", p=128)

    def load_block(wk, k0, k1, c0, cw):
        s = state["slot"] % NSLOT
        state["slot"] += 1
        kc = k1 - k0
        assert kc * cw <= SLOT_ELEMS
        view = ring[:, s, 0:kc * cw].rearrange("p (k c) -> p k c", k=kc)
        src = scr3(wk)[:, k0:k1, c0:c0 + cw]
        deps = cv_ticket[(wk, c0 // cwb(wk))]
        P.dma("sp", lambda h: h.dma_start(out=view, in_=src), "w%d" % s, writes=[tr_slot[s]], deps=deps)
        return s, view

    class LB:
        def __init__(self, *a):
            self.a = a
            self.v = None

        def get(self):
            if self.v is None:
                self.v = load_block(*self.a)
            return self.v

    def run(ths):
        for t in ths:
            t()

    def load_thunks(ti):
        b, t0 = ti // 2, (ti % 2) * T
        dmas, comps = [], []
        for c in range(8):
            def dth(c=c):
                xb = c % NXIN
                P.dma("sp", lambda h: h.dma_start(out=xin[xb][:], in_=x[b, t0 + c * 128:t0 + (c + 1) * 128, :]),
                      "xin%d" % xb, writes=[tr_xin[xb]])

            def th(c=c):
                xb = c % NXIN
                for q in range(2):
                    bk = P.next_bank()
                    fns = [lambda h, d=d, bk=bk: h.transpose(out=banks[bk][:, (d % 4) * 128:(d % 4 + 1) * 128],
                                                            in_=xin[xb][:, d * 128:(d + 1) * 128], identity=ident[:])
                           for d in range(4 * q, 4 * q + 4)]
                    P.emit("pe", fns, reads=[tr_xin[xb], tr_ident], writes=[tr_bank[bk]])
                    P.emit("act", [lambda h, q=q, bk=bk: h.activation(
                        out=xT[:, 4 * q:4 * q + 4, c * 128:(c + 1) * 128],
                        in_=banks[bk][:].rearrange("p (a b) -> p a b", a=4), func=AF.Copy)],
                        reads=[tr_bank[bk]], writes=[tr_xT[d][c // 4] for d in range(4 * q, 4 * q + 4)])
            dmas.append(dth)
            comps.append(th)
        return dmas, comps

    def norm_sq(hh):
        hs = HS(hh)
        for c in range(8):
            P.emit("act", [lambda h, c=c: h.activation(out=sq[:, c, :], in_=xT[:, c, hs], func=AF.Square)],
                   reads=[tr_xT[c][hh]], writes=[tr_sq[c]])

    def norm_mm(gbase, hh, inplace=False):
        hs = HS(hh)
        bk = P.next_bank()
        for c in range(8):
            P.emit("pe", [lambda h, c=c: h.matmul(banks[bk][:], lhsT=onesD[:], rhs=sq[:, c, :], start=(c == 0), stop=(c == 7))],
                   reads=[tr_sq[c], tr_k], writes=[tr_bank[bk]])
        P.emit("act", [lambda h: h.activation(out=rtmp[:], in_=banks[bk][:], func=AF.Sqrt, bias=epsc[:], scale=1.0)],
               reads=[tr_bank[bk], tr_k], writes=[tr_rtmp])
        P.emit("dve", [lambda h: h.reciprocal(out=rstd[hh][:], in_=rtmp[:])], reads=[tr_rtmp], writes=[tr_rstd[hh]])
        for c in range(8):
            if inplace:
                P.emit("dve", [lambda h, c=c: h.scalar_tensor_tensor(
                    out=xT[:, c, hs], in0=xT[:, c, hs], scalar=gcols[:, gbase + c:gbase + c + 1], in1=rstd[hh][:],
                    op0=ALU.mult, op1=ALU.mult)],
                    reads=[tr_xT[c][hh], tr_rstd[hh], tr_gcols], writes=[tr_xT[c][hh]])
            else:
                P.emit("dve", [lambda h, c=c: h.scalar_tensor_tensor(
                    out=hT[:, c, hs], in0=xT[:, c, hs], scalar=gcols[:, gbase + c:gbase + c + 1], in1=rstd[hh][:],
                    op0=ALU.mult, op1=ALU.mult)],
                    reads=[tr_xT[c][hh], tr_rstd[hh], tr_gcols], writes=[tr_hT[c][hh]])

    def mm_fm(wv, mi, kc, rhs_of_k, bk):
        return [lambda h, k=k: h.matmul(banks[bk][:], lhsT=wv[:, k, mi * 128:(mi + 1) * 128], rhs=rhs_of_k(k),
                                        start=(k == 0), stop=(k == kc - 1)) for k in range(kc)]

    def fm_job(lb, mi, kc, rhs_of_k, rd_trs):
        s, wv = lb.get()
        bk = P.next_bank()
        P.emit("pe", mm_fm(wv, mi, kc, rhs_of_k, bk), reads=[tr_slot[s]] + rd_trs, writes=[tr_bank[bk]])
        return bk

    def hT_of(hh):
        hs = HS(hh)
        return (lambda k: hT[:, k, hs]), [tr_hT[k][hh] for k in range(8)]

    def resid_add(bk, m, hh):
        hs = HS(hh)
        P.emit("dve", [lambda h: h.tensor_tensor(out=xT[:, m, hs], in0=banks[bk][:], in1=xT[:, m, hs], op=ALU.add)],
               reads=[tr_bank[bk], tr_xT[m][hh]], writes=[tr_xT[m][hh]])

    def u_jobs(lb, j, hh):
        hs = HS(hh)
        rhs, rds = hT_of(hh)
        out = []
        for mi in range(4):
            def th(mi=mi):
                m = 4 * j + mi
                bk = fm_job(lb, mi, 8, rhs, rds)
                P.emit("act", [lambda h: h.activation(out=uT[:, m, hs], in_=banks[bk][:], func=AF.Gelu_apprx_tanh)],
                       reads=[tr_bank[bk]], writes=[tr_u[m][hh]])
            out.append(th)
        return out

    def v_part(filler=()):
        vs = [load_block("a_in", 0, 8, EA + jj * 512, 512) for jj in range(4)]

        def v_mm(c):
            p = c % 2
            hh = c // 4
            for jj in range(4):
                s, wv = vs[jj]
                bk = P.next_bank()
                fns = [lambda h, k=k, wv=wv, bk=bk: h.matmul(banks[bk][:], lhsT=hT[:, k, c * 128:(c + 1) * 128], rhs=wv[:, k, :],
                                                             start=(k == 0), stop=(k == 7)) for k in range(8)]
                P.emit("pe", fns, reads=[tr_slot[s]] + [tr_hT[k][hh] for k in range(8)], writes=[tr_bank[bk]])
                P.emit("act", [lambda h, jj=jj, bk=bk: h.activation(out=vsb[p][:, jj * 512:(jj + 1) * 512], in_=banks[bk][:],
                                                                   func=AF.Gelu_apprx_tanh, accum_out=vsum[p][:, jj:jj + 1])],
                       reads=[tr_bank[bk]], writes=[tr_vsb[p], tr_vst[p][0]])
            st = vst[p]
            tst = tr_vst[p]
            P.emit("act", [lambda h: h.activation(out=vhat[p][:], in_=vsb[p][:], func=AF.Square, accum_out=st[:, 0:1])],
                   reads=[tr_vsb[p]], writes=[tr_vhat[p], tst[1]])
            P.emit("dve", [lambda h: h.reduce_sum(out=st[:, 1:2], in_=vsum[p][:, 0:4], axis=AX.X)], reads=[tst[0]], writes=[tst[2]])
            P.emit("dve", [lambda h: h.tensor_scalar(out=st[:, 2:3], in0=st[:, 1:2], scalar1=1.0 / EA, scalar2=None, op0=ALU.mult)],
                   reads=[tst[2]], writes=[tst[3]])
            P.emit("dve", [lambda h: h.tensor_tensor(out=st[:, 3:4], in0=st[:, 2:3], in1=st[:, 2:3], op=ALU.mult)],
                   reads=[tst[3]], writes=[tst[4]])
            P.emit("dve", [lambda h: h.scalar_tensor_tensor(out=st[:, 4:5], in0=st[:, 0:1], scalar=1.0 / EA, in1=st[:, 3:4],
                                                            op0=ALU.mult, op1=ALU.subtract)],
                   reads=[tst[1], tst[4]], writes=[tst[5]])
            P.emit("act", [lambda h: h.activation(out=st[:, 5:6], in_=st[:, 4:5], func=AF.Sqrt, bias=epsc[:], scale=1.0)],
                   reads=[tst[5], tr_k], writes=[tst[6]])
            P.emit("dve", [lambda h: h.reciprocal(out=st[:, 6:7], in_=st[:, 5:6])], reads=[tst[6]], writes=[tst[7]])
            P.emit("dve", [lambda h: h.scalar_tensor_tensor(out=st[:, 7:8], in0=st[:, 2:3], scalar=-1.0, in1=st[:, 6:7],
                                                            op0=ALU.mult, op1=ALU.mult)],
                   reads=[tst[3], tst[7]], writes=[tst[0]])
            P.emit("act", [lambda h: h.activation(out=vhat[p][:], in_=vsb[p][:], func=AF.Identity, bias=st[:, 7:8], scale=st[:, 6:7])],
                   reads=[tr_vsb[p], tst[7], tst[0]], writes=[tr_vhat[p]])

        def v_spatial(c):
            p = c % 2
            hh = c // 4
            cs = slice(c * 128, (c + 1) * 128)
            for q in range(4):
                bk = P.next_bank()
                fns = [lambda h, i=i, bk=bk, q=q: h.matmul(banks[bk][:, i * 128:(i + 1) * 128],
                                                      lhsT=vhat[p][:, (4 * q + i) * 128:(4 * q + i + 1) * 128], rhs=wtril[:, 4 * q + i, :],
                                                      start=True, stop=True) for i in range(4)]
                P.emit("pe", fns, reads=[tr_vhat[p], tr_mixa], writes=[tr_bank[bk]])
                yi = state["yt"] % 2
                state["yt"] += 1
                for i in range(4):
                    hd = 4 * q + i
                    P.emit("dve", [lambda h, i=i, hd=hd, bk=bk, yi=yi: h.scalar_tensor_tensor(
                        out=ytmp[yi][:, i * 128:(i + 1) * 128], in0=banks[bk][:, i * 128:(i + 1) * 128],
                        scalar=gcols[:, GC_VG + hd:GC_VG + hd + 1], in1=Bfull[:, hd, :], op0=ALU.mult, op1=ALU.add)],
                        reads=[tr_bank[bk], tr_gcols, tr_mixa], writes=[tr_ytmp[yi]])
                P.emit("dve", [lambda h, q=q, yi=yi: h.tensor_tensor(
                    out=uT[:, 4 * q:4 * q + 4, cs], in0=ytmp[yi][:].rearrange("p (a b) -> p a b", a=4), in1=uT[:, 4 * q:4 * q + 4, cs],
                    op=ALU.mult)],
                    reads=[tr_ytmp[yi]] + [tr_u[4 * q + i][hh] for i in range(4)], writes=[tr_u[4 * q + i][hh] for i in range(4)])

        v_mm(0)
        run(filler)
        for c in range(1, 8):
            v_mm(c)
            v_spatial(c - 1)
        v_spatial(7)

    def outa_jobs(hh):
        hs = HS(hh)
        out = []
        for blk in range(4):
            lb = LB("a_out", 0, 16, blk * 256, 256)
            for mi in range(2):
                def th(lb=lb, blk=blk, mi=mi):
                    m = 2 * blk + mi
                    bk = fm_job(lb, mi, 16, lambda k: uT[:, k, hs], [tr_u[k][hh] for k in range(16)])
                    resid_add(bk, m, hh)
                out.append(th)
        return out

    def ffn_jobs(l, lbg, lbu, c0, cw, hh):
        hs = HS(hh)
        rhs, rds = hT_of(hh)
        out = []
        for mi in range(cw // 128):
            def th(mi=mi):
                m = c0 // 128 + mi
                bg = fm_job(lbg, mi, 8, rhs, rds)
                bu = fm_job(lbu, mi, 8, rhs, rds)
                si = state["sg"] % 3
                state["sg"] += 1
                P.emit("act", [lambda h: h.activation(out=sgbuf[si], in_=banks[bg][:], func=AF.Silu)],
                       reads=[tr_bank[bg]], writes=[tr_sg[si]])
                P.emit("dve", [lambda h: h.tensor_tensor(out=aT[:, m, hs], in0=banks[bu][:], in1=sgbuf[si], op=ALU.mult)],
                       reads=[tr_bank[bu], tr_sg[si]], writes=[tr_a[m][hh]])
            out.append(th)
        return out

    def ffn_rest(l):
        blocks = [(c0, 512) for c0 in range(1024, 2560, 512)] + [(2560, 256)]
        for (c0, cw) in blocks:
            lbg, lbu = LB("g%d" % l, 0, 8, c0, cw), LB("u%d" % l, 0, 8, c0, cw)
            j0 = ffn_jobs(l, lbg, lbu, c0, cw, 0)
            j1 = ffn_jobs(l, lbg, lbu, c0, cw, 1)
            for a_, b_ in zip(j0, j1):
                a_()
                b_()

    def down_jobs(l, hh):
        hs = HS(hh)
        out = []
        for cb in range(4):
            lbA, lbB = LB("d%d" % l, 0, 11, cb * 256, 256), LB("d%d" % l, 11, 22, cb * 256, 256)
            for mi in range(2):
                def th(lbA=lbA, lbB=lbB, cb=cb, mi=mi):
                    m = 2 * cb + mi
                    sA, wA = lbA.get()
                    sB, wB = lbB.get()
                    bk = P.next_bank()
                    fns = [lambda h, k=k: h.matmul(banks[bk][:], lhsT=(wA if k < 11 else wB)[:, k % 11, mi * 128:(mi + 1) * 128],
                                                   rhs=aT[:, k, hs], start=(k == 0), stop=(k == KF - 1)) for k in range(KF)]
                    P.emit("pe", fns, reads=[tr_slot[sA], tr_slot[sB]] + [tr_a[k][hh] for k in range(KF)], writes=[tr_bank[bk]])
                    resid_add(bk, m, hh)
                out.append(th)
        return out

    CW0, CW1, CW2 = GC_CW, GC_CW + 8, GC_CW + 16

    def halo_restore(ti):
        if ti % 2 == 0:
            P.emit("dve", [lambda h: h.memset(zT[:, :, 0:2], 0.0)], writes=tr_z)
        else:
            P.emit("dve", [lambda h: h.tensor_copy(out=zT[:, :, 0:2], in_=zhalo[:])], reads=[tr_zh], writes=tr_z)

    def halo_save():
        P.emit("dve", [lambda h: h.tensor_copy(out=zhalo[:], in_=zT[:, :, T:T + 2])], reads=tr_z, writes=[tr_zh])

    def conv(m):
        cvb = cvbuf[m % 4]
        tcv = tr_cv[m % 4]
        P.emit("dve", [lambda h: h.tensor_scalar(out=cvb, in0=zT[:, m, 2:2 + T], scalar1=gcols[:, CW2 + m:CW2 + m + 1],
                                                 scalar2=None, op0=ALU.mult)],
               reads=[tr_z[m], tr_gcols], writes=[tcv])
        P.emit("dve", [lambda h: h.scalar_tensor_tensor(out=cvb, in0=zT[:, m, 1:1 + T], scalar=gcols[:, CW1 + m:CW1 + m + 1],
                                                        in1=cvb, op0=ALU.mult, op1=ALU.add)],
               reads=[tr_z[m], tcv], writes=[tcv])
        P.emit("dve", [lambda h: h.scalar_tensor_tensor(out=cvb, in0=zT[:, m, 0:T], scalar=gcols[:, CW0 + m:CW0 + m + 1],
                                                        in1=cvb, op0=ALU.mult, op1=ALU.add)],
               reads=[tr_z[m], tcv], writes=[tcv])

    def pair_jobs(pr, lbc, lbh, hh, with_conv):
        rhs, rds = hT_of(hh)
        out = []
        for mi in range(4):
            def th(mi=mi):
                m = 4 * pr + mi
                bc = fm_job(lbc, mi, 8, rhs, rds)
                bh = fm_job(lbh, mi, 8, rhs, rds)
                ci = state["cs"] % 2
                state["cs"] += 1
                P.emit("act", [lambda h: h.activation(out=csb[ci], in_=banks[bc][:], func=AF.Copy)],
                       reads=[tr_bank[bc]], writes=[tr_csb[ci]])
                P.emit("dve", [lambda h: h.tensor_tensor(out=zT[:, m, 2 + hh * 512:2 + (hh + 1) * 512], in0=banks[bh][:], in1=csb[ci],
                                                         op=ALU.mult)],
                       reads=[tr_bank[bh], tr_csb[ci]], writes=[tr_z[m]])
                if with_conv:
                    conv(m)
            out.append(th)
        return out

    def bblock(blk):
        lb = LB("b_in", 0, 8, blk * 512, 512)
        for mi in range(4):
            m = 4 * blk + mi
            for hh in range(2):
                hs = HS(hh)
                rhs, rds = hT_of(hh)
                bk = fm_job(lb, mi, 8, rhs, rds)
                P.emit("dve", [lambda h, bk=bk, m=m, hs=hs: h.tensor_tensor(out=y1T[:, m, hs], in0=banks[bk][:], in1=cvbuf[m % 4][:, hs],
                                                                         op=ALU.mult)],
                       reads=[tr_bank[bk], tr_cv[m % 4]], writes=[tr_y1[m][hh]])

    def outb_jobs(hh):
        hs = HS(hh)
        out = []
        for blk in range(2):
            lb = LB("b_out", 0, 8, blk * 512, 512)
            for mi in range(4):
                def th(lb=lb, blk=blk, mi=mi):
                    m = 4 * blk + mi
                    bk = fm_job(lb, mi, 8, lambda k: y1T[:, k, hs], [tr_y1[k][hh] for k in range(8)])
                    resid_add(bk, m, hh)
                out.append(th)
        return out

    def final_thunks(ti):
        b, t0 = ti // 2, (ti % 2) * T
        return [lambda c=c: final_chunk(b, t0, c) for c in range(8)]

    def final_chunk(b, t0, c):
        hh = c // 4
        pp = state["fin"] % 2
        state["fin"] += 1
        for q in range(2):
            bk = P.next_bank()
            fns = [lambda h, i=i, q=q, bk=bk: h.transpose(out=banks[bk][:, i * 128:(i + 1) * 128],
                                                         in_=xT[:, 4 * q + i, c * 128:(c + 1) * 128], identity=ident[:])
                   for i in range(4)]
            P.emit("pe", fns, reads=[tr_xT[4 * q + i][hh] for i in range(4)] + [tr_ident], writes=[tr_bank[bk]])
            if q == 0:
                P.emit("act", [lambda h, bk=bk: h.activation(out=ost[pp][:, 0:512], in_=banks[bk][:], func=AF.Copy)],
                       reads=[tr_bank[bk]], writes=[tr_ost[pp]])
            else:
                P.emit("act", [lambda h, bk=bk: h.activation(out=ost[pp][:, 512:1024], in_=banks[bk][:], func=AF.Copy)],
                       reads=[tr_bank[bk]], writes=[tr_ost[pp]])
        P.dma("pool", lambda h: h.dma_start(out=y[b, t0 + c * 128:t0 + (c + 1) * 128, :], in_=ost[pp]),
              "ost%d" % pp, reads=[tr_ost[pp]])

    def fence(trs_prev, trs_next):
        agg_r = {}
        for tr in trs_prev:
            for k, v in tr.r.items():
                if v > agg_r.get(k, 0):
                    agg_r[k] = v
            if tr.w is not None:
                k, v = tr.w
                if v > agg_r.get(k, 0):
                    agg_r[k] = v
        for tr in trs_next:
            for k, v in agg_r.items():
                if v > tr.r.get(k, 0):
                    tr.r[k] = v

    def flat(x_):
        out = []
        for e in x_:
            if isinstance(e, (list, tuple)):
                out += flat(e)
            else:
                out.append(e)
        return out

    grp_a = flat([tr_u, tr_vsb, tr_vhat])
    grp_f = flat([tr_a, tr_sg, tr_ost])
    grp_b = flat([tr_z, tr_cv, tr_y1, tr_csb])

    full = (tuple(layers) == (0, 1)) and do_mixer and do_ffn
    assert full or cfg.get("simple", True)
    ldd0, ld0 = load_thunks(0)
    run(ldd0[0:4])
    run(ld0[0:4])
    run(ldd0[4:8])
    norm_sq(0)
    run(ld0[4:8])
    setup_late_dmas()
    first_g = (GC_MIX + 8 * layers[0]) if (layers and do_mixer) else ((GC_FFN + 8 * layers[0]) if layers else 0)
    pending_norm = bool(layers)
    if pending_norm:
        norm_mm(first_g, 0)
        norm_sq(1)
    prev = []
    for ti in range(ntiles):
        last_down = None
        first = True
        nxt = None
        for l in layers:
            if do_mixer:
                gb = GC_MIX + 8 * l
                if not first:
                    raise AssertionError
                if l == 0:
                    fence(prev, grp_a)
                    lbs = [LB("a_in", 0, 8, j * 512, 512) for j in range(3)]
                    uh0 = sum([u_jobs(lbs[j], j, 0) for j in range(3)], [])
                    uh1 = sum([u_jobs(lbs[j], j, 1) for j in range(3)], [])
                    run(uh0[:2])
                    norm_mm(gb, 1)
                    run(uh0[2:])
                    if ti == 0:
                        setup_mixa()
                        fence([tr_stg] + stg_trs, flat([tr_vsb, tr_vhat]))
                    run(uh1)
                    lb3 = LB("a_in", 0, 8, 3 * 512, 512)
                    u3 = [t for pr_ in zip(u_jobs(lb3, 3, 0), u_jobs(lb3, 3, 1)) for t in pr_]
                    v_part(u3)
                    o0, o1 = outa_jobs(0), outa_jobs(1)
                    npre = 3
                    prev = grp_a
                else:
                    fence(prev, grp_b)
                    halo_restore(ti)
                    lbc0, lbh0 = LB("b_in", 0, 8, D, 512), LB("b_in", 0, 8, 2 * D, 512)
                    lbc1, lbh1 = LB("b_in", 0, 8, D + 512, 512), LB("b_in", 0, 8, 2 * D + 512, 512)
                    p0 = pair_jobs(0, lbc0, lbh0, 0, False) + pair_jobs(1, lbc1, lbh1, 0, False)
                    run(p0[:1])
                    norm_mm(gb, 1)
                    run(p0[1:])
                    run(pair_jobs(0, lbc0, lbh0, 1, True))
                    bblock(0)
                    run(pair_jobs(1, lbc1, lbh1, 1, True))
                    bblock(1)
                    halo_save()
                    o0, o1 = outb_jobs(0), outb_jobs(1)
                    npre = 4
                    prev = grp_b
                gf = GC_FFN + 8 * l
                run(o0)
                if do_ffn:
                    norm_sq(0)
                run(o1[:2])
                if do_ffn:
                    norm_mm(gf, 0)
                run(o1[2:])
                if do_ffn:
                    norm_sq(1)
            else:
                gf = GC_FFN + 8 * l
            if do_ffn:
                fence(prev, grp_f)
                prev = grp_f
                if l == layers[-1]:
                    nxt = load_thunks(ti + 1) if ti + 1 < ntiles else (None, None)
                    if nxt[0]:
                        run(nxt[0][0:4])
                lbg0, lbu0 = LB("g%d" % l, 0, 8, 0, 512), LB("u%d" % l, 0, 8, 0, 512)
                lbg1, lbu1 = LB("g%d" % l, 0, 8, 512, 512), LB("u%d" % l, 0, 8, 512, 512)
                f0 = ffn_jobs(l, lbg0, lbu0, 0, 512, 0) + ffn_jobs(l, lbg1, lbu1, 512, 512, 0)
                f1 = ffn_jobs(l, lbg0, lbu0, 0, 512, 1) + ffn_jobs(l, lbg1, lbu1, 512, 512, 1)
                run(f0[:2])
                norm_mm(gf, 1)
                run(f0[2:])
                run(f1)
                ffn_rest(l)
                d0, d1 = down_jobs(l, 0), down_jobs(l, 1)
                if l != layers[-1]:
                    gn = GC_MIX + 8 * layers[layers.index(l) + 1] if do_mixer else GC_FFN + 8 * layers[layers.index(l) + 1]
                    run(d0)
                    norm_sq(0)
                    run(d1[:2])
                    norm_mm(gn, 0)
                    run(d1[2:])
                    norm_sq(1)
                else:
                    last_down = (d0, d1)
            elif l != layers[-1]:
                gn = GC_MIX + 8 * layers[layers.index(l) + 1]
                norm_sq(0)
                norm_mm(gn, 0)
                norm_sq(1)
            first = True
        fence(prev, grp_f)
        prev = grp_f
        fin = final_thunks(ti)
        if nxt is None:
            nxt = load_thunks(ti + 1) if ti + 1 < ntiles else (None, None)
            if nxt[0]:
                run(nxt[0][0:4])
        lddn, ldn = nxt
        d0, d1 = last_down if last_down else ([], [None] * 8)
        d1 = [t for t in d1 if t is not None]
        run(d0)
        if do_final:
            norm_sq(0)
        run(d1[:2])
        if do_final:
            norm_mm(GC_FIN, 0, inplace=True)
        run(d1[2:])
        if do_final:
            norm_sq(1)
        run(fin[0:2])
        if do_final:
            norm_mm(GC_FIN, 1, inplace=True)
        run(fin[2:4])
        if ldn:
            run(ldn[0:4])
            run(lddn[4:8])
            if pending_norm:
                norm_sq(0)
        run(fin[4:6])
        if ldn and pending_norm:
            norm_mm(first_g, 0)
        run(fin[6:8])
        if ldn:
            run(ldn[4:8])
            if pending_norm:
                norm_sq(1)
    fin_t = [(k, P.count[k]) for k in ("ost0", "ost1") if k in P.count]
    P.wait("pool", fin_t)

    with nc.Block() as block:
        @block.tensor
        def _(h):
            for f in P.streams["pe"]:
                f(h)

        @block.scalar
        def _(h):
            for f in P.streams["act"]:
                f(h)

        @block.vector
        def _(h):
            for f in P.streams["dve"]:
                f(h)

        @block.gpsimd
        def _(h):
            for f in P.streams["pool"]:
                f(h)

        @block.sync
        def _(h):
            for f in P.streams["sp"]:
                f(h)
    stack.close()
    return nc


def make_in_maps(inputs, ncores=NCORES):
    f = lambda a: np.ascontiguousarray(np.asarray(a, dtype=np.float32))
    x = f(inputs["x"])
    gc = np.zeros((128, GC_N), np.float32)
    mix = f(inputs["mix_norm"]); ffn = f(inputs["ffn_norm"])
    for l in range(2):
        gc[:, GC_MIX + 8 * l:GC_MIX + 8 * l + 8] = mix[l].reshape(8, 128).T
        gc[:, GC_FFN + 8 * l:GC_FFN + 8 * l + 8] = ffn[l].reshape(8, 128).T
    gc[:, GC_VG:GC_VG + 16] = f(inputs["a_v_gain"])[0].reshape(16, 128).T
    cw = f(inputs["b_conv_w"])[0]
    for k in range(3):
        gc[:, GC_CW + 8 * k:GC_CW + 8 * k + 8] = cw[k].reshape(8, 128).T
    gc[:, GC_FIN:GC_FIN + 8] = f(inputs["final_norm"]).reshape(8, 128).T
    gc[:, GC_VB:GC_VB + 16] = f(inputs["a_v_bias"])[0].reshape(16, 128).T
    ws = f(inputs["a_w_s"])[0]
    wsT = np.ascontiguousarray(ws.transpose(2, 0, 1)).reshape(128, 16 * 128)
    common = {
        "a_w_in": f(inputs["a_w_in"])[0], "a_w_out": f(inputs["a_w_out"])[0],
        "ffn_w_gate0": f(inputs["ffn_w_gate"])[0], "ffn_w_up0": f(inputs["ffn_w_up"])[0], "ffn_w_down0": f(inputs["ffn_w_down"])[0],
        "b_w_in": f(inputs["b_w_in"])[0], "b_w_out": f(inputs["b_w_out"])[0],
        "ffn_w_gate1": f(inputs["ffn_w_gate"])[1], "ffn_w_up1": f(inputs["ffn_w_up"])[1], "ffn_w_down1": f(inputs["ffn_w_down"])[1],
        "ident": np.eye(128, dtype=np.float32),
        "maskT": np.triu(np.ones((128, 128), np.float32)),
        "gcols": gc,
        "a_w_sT": wsT,
        "a_b_s_bc": np.broadcast_to(f(inputs["a_b_s"])[0].reshape(1, EA), (128, EA)),
    }
    common = {k: np.ascontiguousarray(v) for k, v in common.items()}
    maps = []
    for c in range(ncores):
        m = dict(common)
        m["x"] = np.ascontiguousarray(x[c * SEQ_PER_CORE:(c + 1) * SEQ_PER_CORE])
        maps.append(m)
    return maps


def kernel(**inputs):
    nc = build()
    in_maps = make_in_maps(inputs)
    res = run_bass_kernel_spmd(nc, in_maps, core_ids=list(range(NCORES)))
    return np.concatenate([np.asarray(r["y"]) for r in res.results], axis=0).astype(np.float32)
```

```python
import contextlib
import numpy as np
import concourse.bass as bass
import concourse.mybir as mybir
from concourse.bass_utils import run_bass_kernel_spmd

F32 = mybir.dt.float32
BF16 = mybir.dt.bfloat16
AF = mybir.ActivationFunctionType
ALU = mybir.AluOpType
AX = mybir.AxisListType

D = 1024
S = 2048
NCORES = 8
SEQ_PER_CORE = 2
T = 1024
NTILES = SEQ_PER_CORE * S // T
EA = 2048
DFF = 2816
KF = DFF // 128
EPS = 1e-6
NSLOT = 5
NXIN = 4
NIF = 6
SLOT_ELEMS = 4096

GC_MIX = 0
GC_FFN = 16
GC_VG = 32
GC_CW = 48
GC_FIN = 72
GC_VB = 80
GC_N = 96

ENGS = ("pe", "act", "dve", "pool", "sp")


class Tr:
    __slots__ = ("w", "r")

    def __init__(self):
        self.w = None
        self.r = {}


class Prog:
    def __init__(self, nc, stack):
        self.nc = nc
        self.stack = stack
        self.streams = {e: [] for e in ENGS}
        self.sems = {}
        self.count = {}
        self.seen = {e: {} for e in ENGS}
        for e in ENGS:
            self.new_sem(e)
        self.bank_i = 0

    def new_sem(self, key):
        self.sems[key] = self.stack.enter_context(self.nc.semaphore("s_" + key))
        self.count[key] = 0

    def _deps(self, eng, reads, writes, deps):
        need = {}

        def add(t, raw):
            if t is None:
                return
            k, v = t
            if k == eng and (eng == "pe" or not raw):
                return
            if v > need.get(k, 0):
                need[k] = v

        for tr in reads:
            add(tr.w, True)
        for tr in writes:
            add(tr.w, False)
            for k, v in tr.r.items():
                add((k, v), False)
        for t in deps:
            add(t, True)
        out = []
        seen = self.seen[eng]
        for k, v in need.items():
            if v > seen.get(k, 0):
                seen[k] = v
                out.append((k, v))
        return out

    def _update(self, ticket, reads, writes):
        k, v = ticket
        for tr in reads:
            if v > tr.r.get(k, 0):
                tr.r[k] = v
        for tr in writes:
            tr.w = ticket
            tr.r = {}

    def emit(self, eng, fns, reads=(), writes=(), deps=()):
        waits = self._deps(eng, reads, writes, deps)
        self.count[eng] += 1
        ticket = (eng, self.count[eng])
        sems = self.sems

        def run(h, waits=waits, fns=fns, eng=eng):
            for k, v in waits[1:]:
                h.wait_ge(sems[k], v)
            n = len(fns)
            for i, fn in enumerate(fns):
                ins = fn(h)
                if i == 0 and waits:
                    ins._wait_ge(sems[waits[0][0]], waits[0][1])
                if i == n - 1:
                    ins.then_inc(sems[eng], 1)

        self.streams[eng].append(run)
        self._update(ticket, reads, writes)
        return ticket

    def dma(self, eng, fn, sem, reads=(), writes=(), deps=()):
        waits = self._deps(eng, reads, writes, deps)
        if sem not in self.sems:
            self.new_sem(sem)
        self.count[sem] += 16
        ticket = (sem, self.count[sem])
        sems = self.sems

        def run(h, waits=waits, fn=fn, sem=sem):
            for k, v in waits:
                h.wait_ge(sems[k], v)
            fn(h).then_inc(sems[sem], 16)

        self.streams[eng].append(run)
        self._update(ticket, reads, writes)
        return ticket

    def wait(self, eng, tickets):
        waits = self._deps(eng, (), (), tickets)
        sems = self.sems

        def run(h, waits=waits):
            for k, v in waits:
                h.wait_ge(sems[k], v)

        self.streams[eng].append(run)

    def next_bank(self):
        b = self.bank_i % 8
        self.bank_i += 1
        return b


def build(cfg=None):
    cfg = dict(cfg or {})
    ntiles = cfg.get("ntiles", NTILES)
    layers = cfg.get("layers", (0, 1))
    do_mixer = cfg.get("mixer", True)
    do_ffn = cfg.get("ffn", True)
    do_final = cfg.get("final_norm", True)

    nc = bass.Bass("TRN2", target_bir_lowering=False)
    stack = contextlib.ExitStack()

    def din(name, shape, dt=F32):
        return nc.dram_tensor(name, list(shape), dt, kind="ExternalInput").ap()

    x = din("x", [SEQ_PER_CORE, S, D])
    y = nc.dram_tensor("y", [SEQ_PER_CORE, S, D], F32, kind="ExternalOutput").ap()
    wsrc = {
        "a_in": din("a_w_in", [D, 2 * EA]),
        "a_out": din("a_w_out", [EA, D]),
        "g0": din("ffn_w_gate0", [D, DFF]),
        "u0": din("ffn_w_up0", [D, DFF]),
        "d0": din("ffn_w_down0", [DFF, D]),
        "b_in": din("b_w_in", [D, 3 * D]),
        "b_out": din("b_w_out", [D, D]),
        "g1": din("ffn_w_gate1", [D, DFF]),
        "u1": din("ffn_w_up1", [D, DFF]),
        "d1": din("ffn_w_down1", [DFF, D]),
    }
    worder = ["a_in", "a_out", "g0", "u0", "d0", "b_in", "b_out", "g1", "u1", "d1"]
    wscr = {}
    for k in worder:
        K, E = wsrc[k].shape
        wscr[k] = nc.dram_tensor("scr_" + k, [K, E], BF16, kind="Internal").ap()
    ident_d = din("ident", [128, 128])
    mask_d = din("maskT", [128, 128])
    gcols_d = din("gcols", [128, GC_N])
    bsb_d = din("a_b_s_bc", [128, EA])
    wsT_d = din("a_w_sT", [128, 16 * 128])

    def sb(name, shape, dt):
        return stack.enter_context(nc.sbuf_tensor("sb_" + name, list(shape), dt))

    xT = sb("xT", [128, 8, T], F32)
    hT = sb("hT", [128, 8, T], BF16)
    sq = sb("sq", [128, 8, 512], BF16)
    rtmp = sb("rtmp", [128, 512], F32)
    rstd = [sb("rstd%d" % i, [128, 512], F32) for i in range(2)]
    ring = sb("ring", [128, NSLOT, SLOT_ELEMS], BF16)
    xin = [sb("xin%d" % i, [128, D], F32) for i in range(NXIN)]
    ident = sb("ident", [128, 128], F32)
    maskT = sb("maskT", [128, 128], F32)
    onesD = sb("onesD", [128, 128], BF16)
    ones16 = sb("ones16", [128, 128], BF16)
    epsc = sb("epsc", [128, 1], F32)
    gcols = sb("gcols", [128, GC_N], F32)
    wtril = sb("wtril", [128, 16, 128], BF16)
    Bfull = sb("Bfull", [128, 16, 128], F32)
    ytmp = [sb("ytmp%d" % i, [128, 512], F32) for i in range(2)]
    vsum = [sb("vsum%d" % i, [128, 4], F32) for i in range(2)]
    vst = [sb("vst%d" % i, [128, 8], F32) for i in range(2)]
    fst = [sb("fst%d" % i, [128, 8], F32) for i in range(2)]
    zhalo = sb("zhalo", [128, 8, 2], F32)
    SH = 35200
    shared = sb("shared", [128, SH], BF16)

    def shv(off, n, dt):
        ap = shared[:, off:off + n]
        return ap.bitcast(F32) if dt == F32 else ap

    uT = shv(0, 16 * T, BF16).rearrange("p (m t) -> p m t", m=16)
    vsb = [shv(16384 + i * 4096, 4096, F32) for i in range(2)]
    vhat = [shv(24576 + i * 2048, 2048, BF16) for i in range(2)]
    aT = shv(0, KF * T, BF16).rearrange("p (m t) -> p m t", m=KF)
    sgbuf = [shv(22528 + i * 1024, 1024, F32) for i in range(3)]
    ost = [shv(25600 + i * 2048, 2048, F32) for i in range(2)]
    ZW = T + 4
    zT = shv(0, 8 * ZW * 2, F32).rearrange("p (m t) -> p m t", m=8)
    z_end = 8 * ZW * 2
    cvbuf = [shv(z_end + i * 2048, 2048, F32) for i in range(4)]
    y1_off = z_end + 4 * 2048
    y1T = shv(y1_off, 8 * T, BF16).rearrange("p (m t) -> p m t", m=8)
    csb = [shv(y1_off + 8 * T + i * 1024, 1024, F32) for i in range(2)]
    assert y1_off + 8 * T + 2048 <= SH
    bsb = shv(16384, 4096, F32).rearrange("p (h t) -> p h t", h=16)
    wst = shv(24576, 4096, F32).rearrange("p (h t) -> p h t", h=16)

    banks = [stack.enter_context(nc.psum_tensor("ps%d" % i, [128, 512], F32)) for i in range(8)]

    P = Prog(nc, stack)
    tr_bank = [Tr() for _ in range(8)]
    tr_xT = [[Tr() for _ in range(2)] for _ in range(8)]
    tr_hT = [[Tr() for _ in range(2)] for _ in range(8)]
    tr_sq = [Tr() for _ in range(8)]
    tr_rtmp = Tr()
    tr_rscr = Tr()
    tr_rstd = [Tr(), Tr()]
    tr_slot = [Tr() for _ in range(NSLOT)]
    tr_xin = [Tr() for _ in range(NXIN)]
    tr_const = Tr()
    tr_ident, tr_mask, tr_gcols, tr_k, tr_mixa = Tr(), Tr(), Tr(), Tr(), Tr()
    tr_ytmp = [Tr(), Tr()]
    tr_u = [[Tr() for _ in range(2)] for _ in range(16)]
    tr_vsb = [Tr(), Tr()]
    tr_vhat = [Tr(), Tr()]
    tr_vhatB = [Tr(), Tr()]
    tr_vst = [[Tr() for _ in range(8)] for _ in range(2)]
    tr_a = [[Tr() for _ in range(2)] for _ in range(KF)]
    tr_sg = [Tr() for _ in range(3)]
    tr_ost = [Tr(), Tr()]
    tr_z = [Tr() for _ in range(8)]
    tr_cv = [Tr() for _ in range(4)]
    tr_y1 = [[Tr() for _ in range(2)] for _ in range(8)]
    tr_csb = [Tr(), Tr()]
    tr_fst = [[Tr() for _ in range(8)] for _ in range(2)]
    tr_stg = Tr()
    tr_zh = Tr()
    state = {"slot": 0, "sg": 0, "cs": 0, "fin": 0, "yt": 0}

    def HS(h):
        return slice(h * 512, (h + 1) * 512)

    stg_trs = []

    def stg_new():
        t = Tr()
        stg_trs.append(t)
        return t

    def ld(dst, src, sem, t):
        return P.dma("sp", lambda h: h.dma_start(out=dst, in_=src), sem, writes=[t])

    use_a = do_mixer and 0 in layers
    ld(ident[:], ident_d, "c0", tr_ident)
    ld(gcols[:], gcols_d, "c2", tr_gcols)
    P.emit("dve", [lambda h: h.memset(onesD[:], 1.0 / D)], writes=[tr_k])
    P.emit("dve", [lambda h: h.memset(epsc[:], EPS)], writes=[tr_k])

    def setup_late_dmas():
        if use_a:
            ld(maskT[:], mask_d, "c1", tr_mask)
            P.dma("sp", lambda h: h.dma_start(out=wst, in_=wsT_d.rearrange("p (h t) -> p h t", h=16)), "c4", writes=[stg_new()])
            P.dma("sp", lambda h: h.dma_start(out=bsb, in_=bsb_d.rearrange("p (h t) -> p h t", h=16)), "c5", writes=[stg_new()])

    def setup_mixa():
        for hd in range(16):
            P.emit("dve", [lambda h, hd=hd: h.tensor_tensor(out=wtril[:, hd, :], in0=wst[:, hd, :], in1=maskT[:], op=ALU.mult)],
                   reads=stg_trs + [tr_stg, tr_mask], writes=[tr_mixa])
        P.emit("dve", [lambda h: h.memset(ones16[:], 1.0)], writes=[tr_mixa])
        for q in range(4):
            bk = P.next_bank()
            P.emit("pe", [lambda h, q=q, bk=bk: h.matmul(banks[bk][:], lhsT=ones16[:],
                                                         rhs=wtril[:].rearrange("p h t -> p (h t)")[:, q * 512:(q + 1) * 512], start=True, stop=True)],
                   reads=[tr_mixa], writes=[tr_bank[bk]])
            for i in range(4):
                hd = 4 * q + i
                P.emit("dve", [lambda h, i=i, hd=hd, bk=bk: h.scalar_tensor_tensor(
                    out=Bfull[:, hd, :], in0=banks[bk][:, i * 128:(i + 1) * 128], scalar=gcols[:, GC_VB + hd:GC_VB + hd + 1],
                    in1=bsb[:, hd, :], op0=ALU.mult, op1=ALU.add)],
                    reads=stg_trs + [tr_bank[bk], tr_gcols], writes=[tr_mixa, tr_stg])

    cv_ticket = {}
    pp_state = {"i": 0}
    pp_last = {}

    def pp_dma(fn):
        j = pp_state["i"] % NIF
        pp_state["i"] += 1
        sem = "pp%d" % j
        deps = [pp_last[sem]] if sem in pp_last else []
        pp_last[sem] = P.dma("pool", fn, sem, deps=deps)

    used = set()
    if do_mixer and 0 in layers:
        used |= {"a_in", "a_out"}
    if do_mixer and 1 in layers:
        used |= {"b_in", "b_out"}
    if do_ffn:
        for l in layers:
            used |= {"g%d" % l, "u%d" % l, "d%d" % l}
    def cwb(k):
        return 256 if (k == "a_out" or k[0] == "d") else 512

    for k in worder:
        if k not in used:
            continue
        K, E = wsrc[k].shape
        for j in range((E + cwb(k) - 1) // cwb(k)):
            c0, c1 = j * cwb(k), min(E, (j + 1) * cwb(k))
            for kc in range(K // 128):
                pp_dma(lambda h, k=k, kc=kc, c0=c0, c1=c1: h.dma_start(out=wscr[k][kc * 128:(kc + 1) * 128, c0:c1],
                                                                       in_=wsrc[k][kc * 128:(kc + 1) * 128, c0:c1]))
            cv_ticket[(k, j)] = list(pp_last.values())

    def scr3(k):
        return wscr[k].rearrange("(k p) e -> p k e", p=128)

    def load_block(wk, k0, k1, c0, cw):
        s = state["slot"] % NSLOT
        state["slot"] += 1
        kc = k1 - k0
        assert kc * cw <= SLOT_ELEMS
        view = ring[:, s, 0:kc * cw].rearrange("p (k c) -> p k c", k=kc)
        src = scr3(wk)[:, k0:k1, c0:c0 + cw]
        deps = cv_ticket[(wk, c0 // cwb(wk))]
        P.dma("sp", lambda h: h.dma_start(out=view, in_=src), "w%d" % s, writes=[tr_slot[s]], deps=deps)
        return s, view

    class LB:
        def __init__(self, *a):
            self.a = a
            self.v = None

        def get(self):
            if self.v is None:
                self.v = load_block(*self.a)
            return self.v

    def run(ths):
        for t in ths:
            t()

    def load_thunks(ti):
        b, t0 = ti // 2, (ti % 2) * T
        dmas, comps = [], []
        for c in range(8):
            def dth(c=c):
                xb = c % NXIN
                P.dma("sp", lambda h: h.dma_start(out=xin[xb][:], in_=x[b, t0 + c * 128:t0 + (c + 1) * 128, :]),
                      "xin%d" % xb, writes=[tr_xin[xb]])

            def th(c=c):
                xb = c % NXIN
                for q in range(2):
                    bk = P.next_bank()
                    fns = [lambda h, d=d, bk=bk: h.transpose(out=banks[bk][:, (d % 4) * 128:(d % 4 + 1) * 128],
                                                            in_=xin[xb][:, d * 128:(d + 1) * 128], identity=ident[:])
                           for d in range(4 * q, 4 * q + 4)]
                    P.emit("pe", fns, reads=[tr_xin[xb], tr_ident], writes=[tr_bank[bk]])
                    P.emit("act", [lambda h, q=q, bk=bk: h.activation(
                        out=xT[:, 4 * q:4 * q + 4, c * 128:(c + 1) * 128],
                        in_=banks[bk][:].rearrange("p (a b) -> p a b", a=4), func=AF.Copy)],
                        reads=[tr_bank[bk]], writes=[tr_xT[d][c // 4] for d in range(4 * q, 4 * q + 4)])
            dmas.append(dth)
            comps.append(th)
        return dmas, comps

    def norm_sq(hh):
        hs = HS(hh)
        for c in range(8):
            P.emit("act", [lambda h, c=c: h.activation(out=sq[:, c, :], in_=xT[:, c, hs], func=AF.Square)],
                   reads=[tr_xT[c][hh]], writes=[tr_sq[c]])

    def norm_mm(gbase, hh, inplace=False):
        hs = HS(hh)
        bk = P.next_bank()
        for c in range(8):
            P.emit("pe", [lambda h, c=c: h.matmul(banks[bk][:], lhsT=onesD[:], rhs=sq[:, c, :], start=(c == 0), stop=(c == 7))],
                   reads=[tr_sq[c], tr_k], writes=[tr_bank[bk]])
        P.emit("act", [lambda h: h.activation(out=rtmp[:], in_=banks[bk][:], func=AF.Sqrt, bias=epsc[:], scale=1.0)],
               reads=[tr_bank[bk], tr_k], writes=[tr_rtmp])
        P.emit("dve", [lambda h: h.reciprocal(out=rstd[hh][:], in_=rtmp[:])], reads=[tr_rtmp], writes=[tr_rstd[hh]])
        for c in range(8):
            if inplace:
                P.emit("dve", [lambda h, c=c: h.scalar_tensor_tensor(
                    out=xT[:, c, hs], in0=xT[:, c, hs], scalar=gcols[:, gbase + c:gbase + c + 1], in1=rstd[hh][:],
                    op0=ALU.mult, op1=ALU.mult)],
                    reads=[tr_xT[c][hh], tr_rstd[hh], tr_gcols], writes=[tr_xT[c][hh]])
            else:
                P.emit("dve", [lambda h, c=c: h.scalar_tensor_tensor(
                    out=hT[:, c, hs], in0=xT[:, c, hs], scalar=gcols[:, gbase + c:gbase + c + 1], in1=rstd[hh][:],
                    op0=ALU.mult, op1=ALU.mult)],
                    reads=[tr_xT[c][hh], tr_rstd[hh], tr_gcols], writes=[tr_hT[c][hh]])

    def mm_fm(wv, mi, kc, rhs_of_k, bk):
        return [lambda h, k=k: h.matmul(banks[bk][:], lhsT=wv[:, k, mi * 128:(mi + 1) * 128], rhs=rhs_of_k(k),
                                        start=(k == 0), stop=(k == kc - 1)) for k in range(kc)]

    def fm_job(lb, mi, kc, rhs_of_k, rd_trs):
        s, wv = lb.get()
        bk = P.next_bank()
        P.emit("pe", mm_fm(wv, mi, kc, rhs_of_k, bk), reads=[tr_slot[s]] + rd_trs, writes=[tr_bank[bk]])
        return bk

    def hT_of(hh):
        hs = HS(hh)
        return (lambda k: hT[:, k, hs]), [tr_hT[k][hh] for k in range(8)]

    def resid_add(bk, m, hh):
        hs = HS(hh)
        P.emit("dve", [lambda h: h.tensor_tensor(out=xT[:, m, hs], in0=banks[bk][:], in1=xT[:, m, hs], op=ALU.add)],
               reads=[tr_bank[bk], tr_xT[m][hh]], writes=[tr_xT[m][hh]])

    def u_jobs(lb, j, hh):
        hs = HS(hh)
        rhs, rds = hT_of(hh)
        out = []
        for mi in range(4):
            def th(mi=mi):
                m = 4 * j + mi
                bk = fm_job(lb, mi, 8, rhs, rds)
                P.emit("act", [lambda h: h.activation(out=uT[:, m, hs], in_=banks[bk][:], func=AF.Gelu_apprx_tanh)],
                       reads=[tr_bank[bk]], writes=[tr_u[m][hh]])
            out.append(th)
        return out

    def v_part(filler=()):
        vs = [load_block("a_in", 0, 8, EA + jj * 512, 512) for jj in range(4)]

        def v_mm(c):
            p = c % 2
            hh = c // 4
            for jj in range(4):
                s, wv = vs[jj]
                bk = P.next_bank()
                fns = [lambda h, k=k, wv=wv, bk=bk: h.matmul(banks[bk][:], lhsT=hT[:, k, c * 128:(c + 1) * 128], rhs=wv[:, k, :],
                                                             start=(k == 0), stop=(k == 7)) for k in range(8)]
                P.emit("pe", fns, reads=[tr_slot[s]] + [tr_hT[k][hh] for k in range(8)], writes=[tr_bank[bk]])
                P.emit("act", [lambda h, jj=jj, bk=bk: h.activation(out=vsb[p][:, jj * 512:(jj + 1) * 512], in_=banks[bk][:],
                                                                   func=AF.Gelu_apprx_tanh, accum_out=vsum[p][:, jj:jj + 1])],
                       reads=[tr_bank[bk]], writes=[tr_vsb[p], tr_vst[p][0]])
            st = vst[p]
            tst = tr_vst[p]
            P.emit("act", [lambda h: h.activation(out=vhat[p][:], in_=vsb[p][:], func=AF.Square, accum_out=st[:, 0:1])],
                   reads=[tr_vsb[p]], writes=[tr_vhat[p], tr_vhatB[p], tst[1]])
            P.emit("dve", [lambda h: h.reduce_sum(out=st[:, 1:2], in_=vsum[p][:, 0:4], axis=AX.X)], reads=[tst[0]], writes=[tst[2]])
            P.emit("dve", [lambda h: h.tensor_scalar(out=st[:, 2:3], in0=st[:, 1:2], scalar1=1.0 / EA, scalar2=None, op0=ALU.mult)],
                   reads=[tst[2]], writes=[tst[3]])
            P.emit("dve", [lambda h: h.tensor_tensor(out=st[:, 3:4], in0=st[:, 2:3], in1=st[:, 2:3], op=ALU.mult)],
                   reads=[tst[3]], writes=[tst[4]])
            P.emit("dve", [lambda h: h.scalar_tensor_tensor(out=st[:, 4:5], in0=st[:, 0:1], scalar=1.0 / EA, in1=st[:, 3:4],
                                                            op0=ALU.mult, op1=ALU.subtract)],
                   reads=[tst[1], tst[4]], writes=[tst[5]])
            P.emit("act", [lambda h: h.activation(out=st[:, 5:6], in_=st[:, 4:5], func=AF.Sqrt, bias=epsc[:], scale=1.0)],
                   reads=[tst[5], tr_k], writes=[tst[6]])
            P.emit("dve", [lambda h: h.reciprocal(out=st[:, 6:7], in_=st[:, 5:6])], reads=[tst[6]], writes=[tst[7]])
            P.emit("dve", [lambda h: h.scalar_tensor_tensor(out=st[:, 7:8], in0=st[:, 2:3], scalar=-1.0, in1=st[:, 6:7],
                                                            op0=ALU.mult, op1=ALU.mult)],
                   reads=[tst[3], tst[7]], writes=[tst[0]])
            P.emit("act", [lambda h: h.activation(out=vhat[p][:, 0:1024], in_=vsb[p][:, 0:1024], func=AF.Identity, bias=st[:, 7:8], scale=st[:, 6:7])],
                   reads=[tr_vsb[p], tst[7], tst[0]], writes=[tr_vhat[p]])
            P.emit("dve", [lambda h: h.tensor_scalar(out=vhat[p][:, 1024:2048], in0=vsb[p][:, 1024:2048], scalar1=st[:, 2:3], scalar2=st[:, 6:7],
                                                     op0=ALU.subtract, op1=ALU.mult)],
                   reads=[tr_vsb[p], tst[3], tst[7]], writes=[tr_vhatB[p]])

        def v_spatial(c):
            p = c % 2
            hh = c // 4
            cs = slice(c * 128, (c + 1) * 128)
            for q in range(4):
                bk = P.next_bank()
                fns = [lambda h, i=i, bk=bk, q=q: h.matmul(banks[bk][:, i * 128:(i + 1) * 128],
                                                      lhsT=vhat[p][:, (4 * q + i) * 128:(4 * q + i + 1) * 128], rhs=wtril[:, 4 * q + i, :],
                                                      start=True, stop=True) for i in range(4)]
                P.emit("pe", fns, reads=[tr_vhat[p], tr_vhatB[p], tr_mixa], writes=[tr_bank[bk]])
                yi = state["yt"] % 2
                state["yt"] += 1
                for i in range(4):
                    hd = 4 * q + i
                    P.emit("dve", [lambda h, i=i, hd=hd, bk=bk, yi=yi: h.scalar_tensor_tensor(
                        out=ytmp[yi][:, i * 128:(i + 1) * 128], in0=banks[bk][:, i * 128:(i + 1) * 128],
                        scalar=gcols[:, GC_VG + hd:GC_VG + hd + 1], in1=Bfull[:, hd, :], op0=ALU.mult, op1=ALU.add)],
                        reads=[tr_bank[bk], tr_gcols, tr_mixa], writes=[tr_ytmp[yi]])
                P.emit("dve", [lambda h, q=q, yi=yi: h.tensor_tensor(
                    out=uT[:, 4 * q:4 * q + 4, cs], in0=ytmp[yi][:].rearrange("p (a b) -> p a b", a=4), in1=uT[:, 4 * q:4 * q + 4, cs],
                    op=ALU.mult)],
                    reads=[tr_ytmp[yi]] + [tr_u[4 * q + i][hh] for i in range(4)], writes=[tr_u[4 * q + i][hh] for i in range(4)])

        v_mm(0)
        run(filler)
        for c in range(1, 8):
            v_mm(c)
            v_spatial(c - 1)
        v_spatial(7)

    def outa_jobs(hh):
        hs = HS(hh)
        out = []
        for blk in range(4):
            lb = LB("a_out", 0, 16, blk * 256, 256)
            for mi in range(2):
                def th(lb=lb, blk=blk, mi=mi):
                    m = 2 * blk + mi
                    bk = fm_job(lb, mi, 16, lambda k: uT[:, k, hs], [tr_u[k][hh] for k in range(16)])
                    resid_add(bk, m, hh)
                out.append(th)
        return out

    def ffn_jobs(l, lbg, lbu, c0, cw, hh):
        hs = HS(hh)
        rhs, rds = hT_of(hh)
        out = []
        for mi in range(cw // 128):
            def th(mi=mi):
                m = c0 // 128 + mi
                bg = fm_job(lbg, mi, 8, rhs, rds)
                bu = fm_job(lbu, mi, 8, rhs, rds)
                si = state["sg"] % 3
                state["sg"] += 1
                P.emit("act", [lambda h: h.activation(out=sgbuf[si], in_=banks[bg][:], func=AF.Silu)],
                       reads=[tr_bank[bg]], writes=[tr_sg[si]])
                P.emit("dve", [lambda h: h.tensor_tensor(out=aT[:, m, hs], in0=banks[bu][:], in1=sgbuf[si], op=ALU.mult)],
                       reads=[tr_bank[bu], tr_sg[si]], writes=[tr_a[m][hh]])
            out.append(th)
        return out

    def ffn_rest(l):
        blocks = [(c0, 512) for c0 in range(1024, 2560, 512)] + [(2560, 256)]
        for (c0, cw) in blocks:
            lbg, lbu = LB("g%d" % l, 0, 8, c0, cw), LB("u%d" % l, 0, 8, c0, cw)
            j0 = ffn_jobs(l, lbg, lbu, c0, cw, 0)
            j1 = ffn_jobs(l, lbg, lbu, c0, cw, 1)
            for a_, b_ in zip(j0, j1):
                a_()
                b_()

    def down_jobs(l, hh):
        hs = HS(hh)
        out = []
        for cb in range(4):
            lbA, lbB = LB("d%d" % l, 0, 11, cb * 256, 256), LB("d%d" % l, 11, 22, cb * 256, 256)
            for mi in range(2):
                def th(lbA=lbA, lbB=lbB, cb=cb, mi=mi):
                    m = 2 * cb + mi
                    sA, wA = lbA.get()
                    sB, wB = lbB.get()
                    bk = P.next_bank()
                    fns = [lambda h, k=k: h.matmul(banks[bk][:], lhsT=(wA if k < 11 else wB)[:, k % 11, mi * 128:(mi + 1) * 128],
                                                   rhs=aT[:, k, hs], start=(k == 0), stop=(k == KF - 1)) for k in range(KF)]
                    P.emit("pe", fns, reads=[tr_slot[sA], tr_slot[sB]] + [tr_a[k][hh] for k in range(KF)], writes=[tr_bank[bk]])
                    resid_add(bk, m, hh)
                out.append(th)
        return out

    CW0, CW1, CW2 = GC_CW, GC_CW + 8, GC_CW + 16

    def halo_restore(ti):
        if ti % 2 == 0:
            P.emit("dve", [lambda h: h.memset(zT[:, :, 0:2], 0.0)], writes=tr_z)
        else:
            P.emit("dve", [lambda h: h.tensor_copy(out=zT[:, :, 0:2], in_=zhalo[:])], reads=[tr_zh], writes=tr_z)

    def halo_save():
        P.emit("dve", [lambda h: h.tensor_copy(out=zhalo[:], in_=zT[:, :, T:T + 2])], reads=tr_z, writes=[tr_zh])

    def conv(m):
        cvb = cvbuf[m % 4]
        tcv = tr_cv[m % 4]
        P.emit("dve", [lambda h: h.tensor_scalar(out=cvb, in0=zT[:, m, 2:2 + T], scalar1=gcols[:, CW2 + m:CW2 + m + 1],
                                                 scalar2=None, op0=ALU.mult)],
               reads=[tr_z[m], tr_gcols], writes=[tcv])
        P.emit("dve", [lambda h: h.scalar_tensor_tensor(out=cvb, in0=zT[:, m, 1:1 + T], scalar=gcols[:, CW1 + m:CW1 + m + 1],
                                                        in1=cvb, op0=ALU.mult, op1=ALU.add)],
               reads=[tr_z[m], tcv], writes=[tcv])
        P.emit("dve", [lambda h: h.scalar_tensor_tensor(out=cvb, in0=zT[:, m, 0:T], scalar=gcols[:, CW0 + m:CW0 + m + 1],
                                                        in1=cvb, op0=ALU.mult, op1=ALU.add)],
               reads=[tr_z[m], tcv], writes=[tcv])

    def pair_jobs(pr, lbc, lbh, hh, with_conv):
        rhs, rds = hT_of(hh)
        out = []
        for mi in range(4):
            def th(mi=mi):
                m = 4 * pr + mi
                bc = fm_job(lbc, mi, 8, rhs, rds)
                bh = fm_job(lbh, mi, 8, rhs, rds)
                ci = state["cs"] % 2
                state["cs"] += 1
                P.emit("act", [lambda h: h.activation(out=csb[ci], in_=banks[bc][:], func=AF.Copy)],
                       reads=[tr_bank[bc]], writes=[tr_csb[ci]])
                P.emit("dve", [lambda h: h.tensor_tensor(out=zT[:, m, 2 + hh * 512:2 + (hh + 1) * 512], in0=banks[bh][:], in1=csb[ci],
                                                         op=ALU.mult)],
                       reads=[tr_bank[bh], tr_csb[ci]], writes=[tr_z[m]])
                if with_conv:
                    conv(m)
            out.append(th)
        return out

    def bblock(blk):
        lb = LB("b_in", 0, 8, blk * 512, 512)
        for mi in range(4):
            m = 4 * blk + mi
            for hh in range(2):
                hs = HS(hh)
                rhs, rds = hT_of(hh)
                bk = fm_job(lb, mi, 8, rhs, rds)
                P.emit("dve", [lambda h, bk=bk, m=m, hs=hs: h.tensor_tensor(out=y1T[:, m, hs], in0=banks[bk][:], in1=cvbuf[m % 4][:, hs],
                                                                         op=ALU.mult)],
                       reads=[tr_bank[bk], tr_cv[m % 4]], writes=[tr_y1[m][hh]])

    def outb_jobs(hh):
        hs = HS(hh)
        out = []
        for blk in range(2):
            lb = LB("b_out", 0, 8, blk * 512, 512)
            for mi in range(4):
                def th(lb=lb, blk=blk, mi=mi):
                    m = 4 * blk + mi
                    bk = fm_job(lb, mi, 8, lambda k: y1T[:, k, hs], [tr_y1[k][hh] for k in range(8)])
                    resid_add(bk, m, hh)
                out.append(th)
        return out

    def final_thunks(ti):
        b, t0 = ti // 2, (ti % 2) * T
        return [lambda c=c: final_chunk(b, t0, c) for c in range(8)]

    def final_chunk(b, t0, c):
        hh = c // 4
        pp = state["fin"] % 2
        state["fin"] += 1
        for q in range(2):
            bk = P.next_bank()
            fns = [lambda h, i=i, q=q, bk=bk: h.transpose(out=banks[bk][:, i * 128:(i + 1) * 128],
                                                         in_=xT[:, 4 * q + i, c * 128:(c + 1) * 128], identity=ident[:])
                   for i in range(4)]
            P.emit("pe", fns, reads=[tr_xT[4 * q + i][hh] for i in range(4)] + [tr_ident], writes=[tr_bank[bk]])
            if q == 0:
                P.emit("act", [lambda h, bk=bk: h.activation(out=ost[pp][:, 0:512], in_=banks[bk][:], func=AF.Copy)],
                       reads=[tr_bank[bk]], writes=[tr_ost[pp]])
            else:
                P.emit("act", [lambda h, bk=bk: h.activation(out=ost[pp][:, 512:1024], in_=banks[bk][:], func=AF.Copy)],
                       reads=[tr_bank[bk]], writes=[tr_ost[pp]])
        P.dma("pool", lambda h: h.dma_start(out=y[b, t0 + c * 128:t0 + (c + 1) * 128, :], in_=ost[pp]),
              "ost%d" % pp, reads=[tr_ost[pp]])

    def fence(trs_prev, trs_next):
        agg_r = {}
        for tr in trs_prev:
            for k, v in tr.r.items():
                if v > agg_r.get(k, 0):
                    agg_r[k] = v
            if tr.w is not None:
                k, v = tr.w
                if v > agg_r.get(k, 0):
                    agg_r[k] = v
        for tr in trs_next:
            for k, v in agg_r.items():
                if v > tr.r.get(k, 0):
                    tr.r[k] = v

    def flat(x_):
        out = []
        for e in x_:
            if isinstance(e, (list, tuple)):
                out += flat(e)
            else:
                out.append(e)
        return out

    grp_a = flat([tr_u, tr_vsb, tr_vhat, tr_vhatB])
    grp_f = flat([tr_a, tr_sg, tr_ost])
    grp_b = flat([tr_z, tr_cv, tr_y1, tr_csb])

    full = (tuple(layers) == (0, 1)) and do_mixer and do_ffn
    assert full or cfg.get("simple", True)
    ldd0, ld0 = load_thunks(0)
    run(ldd0[0:4])
    run(ld0[0:4])
    run(ldd0[4:8])
    norm_sq(0)
    run(ld0[4:8])
    setup_late_dmas()
    first_g = (GC_MIX + 8 * layers[0]) if (layers and do_mixer) else ((GC_FFN + 8 * layers[0]) if layers else 0)
    pending_norm = bool(layers)
    if pending_norm:
        norm_mm(first_g, 0)
        norm_sq(1)
    prev = []
    for ti in range(ntiles):
        last_down = None
        first = True
        nxt = None
        for l in layers:
            if do_mixer:
                gb = GC_MIX + 8 * l
                if not first:
                    raise AssertionError
                if l == 0:
                    fence(prev, grp_a)
                    lbs = [LB("a_in", 0, 8, j * 512, 512) for j in range(3)]
                    uh0 = sum([u_jobs(lbs[j], j, 0) for j in range(3)], [])
                    uh1 = sum([u_jobs(lbs[j], j, 1) for j in range(3)], [])
                    run(uh0[:2])
                    norm_mm(gb, 1)
                    run(uh0[2:])
                    if ti == 0:
                        setup_mixa()
                        fence([tr_stg] + stg_trs, flat([tr_vsb, tr_vhat, tr_vhatB]))
                    run(uh1)
                    lb3 = LB("a_in", 0, 8, 3 * 512, 512)
                    u3 = [t for pr_ in zip(u_jobs(lb3, 3, 0), u_jobs(lb3, 3, 1)) for t in pr_]
                    v_part(u3)
                    o0, o1 = outa_jobs(0), outa_jobs(1)
                    npre = 3
                    prev = grp_a
                else:
                    fence(prev, grp_b)
                    halo_restore(ti)
                    lbc0, lbh0 = LB("b_in", 0, 8, D, 512), LB("b_in", 0, 8, 2 * D, 512)
                    lbc1, lbh1 = LB("b_in", 0, 8, D + 512, 512), LB("b_in", 0, 8, 2 * D + 512, 512)
                    p0 = pair_jobs(0, lbc0, lbh0, 0, False) + pair_jobs(1, lbc1, lbh1, 0, False)
                    run(p0[:1])
                    norm_mm(gb, 1)
                    run(p0[1:])
                    run(pair_jobs(0, lbc0, lbh0, 1, True))
                    bblock(0)
                    run(pair_jobs(1, lbc1, lbh1, 1, True))
                    bblock(1)
                    halo_save()
                    o0, o1 = outb_jobs(0), outb_jobs(1)
                    npre = 4
                    prev = grp_b
                gf = GC_FFN + 8 * l
                run(o0)
                if do_ffn:
                    norm_sq(0)
                run(o1[:2])
                if do_ffn:
                    norm_mm(gf, 0)
                run(o1[2:])
                if do_ffn:
                    norm_sq(1)
            else:
                gf = GC_FFN + 8 * l
            if do_ffn:
                fence(prev, grp_f)
                prev = grp_f
                if l == layers[-1]:
                    nxt = load_thunks(ti + 1) if ti + 1 < ntiles else (None, None)
                    if nxt[0]:
                        run(nxt[0][0:4])
                lbg0, lbu0 = LB("g%d" % l, 0, 8, 0, 512), LB("u%d" % l, 0, 8, 0, 512)
                lbg1, lbu1 = LB("g%d" % l, 0, 8, 512, 512), LB("u%d" % l, 0, 8, 512, 512)
                f0 = ffn_jobs(l, lbg0, lbu0, 0, 512, 0) + ffn_jobs(l, lbg1, lbu1, 512, 512, 0)
                f1 = ffn_jobs(l, lbg0, lbu0, 0, 512, 1) + ffn_jobs(l, lbg1, lbu1, 512, 512, 1)
                run(f0[:2])
                norm_mm(gf, 1)
                run(f0[2:])
                run(f1)
                ffn_rest(l)
                d0, d1 = down_jobs(l, 0), down_jobs(l, 1)
                if l != layers[-1]:
                    gn = GC_MIX + 8 * layers[layers.index(l) + 1] if do_mixer else GC_FFN + 8 * layers[layers.index(l) + 1]
                    run(d0)
                    norm_sq(0)
                    run(d1[:2])
                    norm_mm(gn, 0)
                    run(d1[2:])
                    norm_sq(1)
                else:
                    last_down = (d0, d1)
            elif l != layers[-1]:
                gn = GC_MIX + 8 * layers[layers.index(l) + 1]
                norm_sq(0)
                norm_mm(gn, 0)
                norm_sq(1)
            first = True
        fence(prev, grp_f)
        prev = grp_f
        fin = final_thunks(ti)
        if nxt is None:
            nxt = load_thunks(ti + 1) if ti + 1 < ntiles else (None, None)
            if nxt[0]:
                run(nxt[0][0:4])
        lddn, ldn = nxt
        d0, d1 = last_down if last_down else ([], [None] * 8)
        d1 = [t for t in d1 if t is not None]
        run(d0)
        if do_final:
            norm_sq(0)
        run(d1[:2])
        if do_final:
            norm_mm(GC_FIN, 0, inplace=True)
        run(d1[2:])
        if do_final:
            norm_sq(1)
        run(fin[0:2])
        if do_final:
            norm_mm(GC_FIN, 1, inplace=True)
        run(fin[2:4])
        if ldn:
            run(ldn[0:4])
            run(lddn[4:8])
            if pending_norm:
                norm_sq(0)
        run(fin[4:6])
        if ldn and pending_norm:
            norm_mm(first_g, 0)
        run(fin[6:8])
        if ldn:
            run(ldn[4:8])
            if pending_norm:
                norm_sq(1)
    fin_t = [(k, P.count[k]) for k in ("ost0", "ost1") if k in P.count]
    P.wait("pool", fin_t)

    with nc.Block() as block:
        @block.tensor
        def _(h):
            for f in P.streams["pe"]:
                f(h)

        @block.scalar
        def _(h):
            for f in P.streams["act"]:
                f(h)

        @block.vector
        def _(h):
            for f in P.streams["dve"]:
                f(h)

        @block.gpsimd
        def _(h):
            for f in P.streams["pool"]:
                f(h)

        @block.sync
        def _(h):
            for f in P.streams["sp"]:
                f(h)
    stack.close()
    return nc


def make_in_maps(inputs, ncores=NCORES):
    f = lambda a: np.ascontiguousarray(np.asarray(a, dtype=np.float32))
    x = f(inputs["x"])
    gc = np.zeros((128, GC_N), np.float32)
    mix = f(inputs["mix_norm"]); ffn = f(inputs["ffn_norm"])
    for l in range(2):
        gc[:, GC_MIX + 8 * l:GC_MIX + 8 * l + 8] = mix[l].reshape(8, 128).T
        gc[:, GC_FFN + 8 * l:GC_FFN + 8 * l + 8] = ffn[l].reshape(8, 128).T
    gc[:, GC_VG:GC_VG + 16] = f(inputs["a_v_gain"])[0].reshape(16, 128).T
    cw = f(inputs["b_conv_w"])[0]
    for k in range(3):
        gc[:, GC_CW + 8 * k:GC_CW + 8 * k + 8] = cw[k].reshape(8, 128).T
    gc[:, GC_FIN:GC_FIN + 8] = f(inputs["final_norm"]).reshape(8, 128).T
    gc[:, GC_VB:GC_VB + 16] = f(inputs["a_v_bias"])[0].reshape(16, 128).T
    ws = f(inputs["a_w_s"])[0]
    wsT = np.ascontiguousarray(ws.transpose(2, 0, 1)).reshape(128, 16 * 128)
    common = {
        "a_w_in": f(inputs["a_w_in"])[0], "a_w_out": f(inputs["a_w_out"])[0],
        "ffn_w_gate0": f(inputs["ffn_w_gate"])[0], "ffn_w_up0": f(inputs["ffn_w_up"])[0], "ffn_w_down0": f(inputs["ffn_w_down"])[0],
        "b_w_in": f(inputs["b_w_in"])[0], "b_w_out": f(inputs["b_w_out"])[0],
        "ffn_w_gate1": f(inputs["ffn_w_gate"])[1], "ffn_w_up1": f(inputs["ffn_w_up"])[1], "ffn_w_down1": f(inputs["ffn_w_down"])[1],
        "ident": np.eye(128, dtype=np.float32),
        "maskT": np.triu(np.ones((128, 128), np.float32)),
        "gcols": gc,
        "a_w_sT": wsT,
        "a_b_s_bc": np.broadcast_to(f(inputs["a_b_s"])[0].reshape(1, EA), (128, EA)),
    }
    common = {k: np.ascontiguousarray(v) for k, v in common.items()}
    maps = []
    for c in range(ncores):
        m = dict(common)
        m["x"] = np.ascontiguousarray(x[c * SEQ_PER_CORE:(c + 1) * SEQ_PER_CORE])
        maps.append(m)
    return maps


def kernel(**inputs):
    nc = build()
    in_maps = make_in_maps(inputs)
    res = run_bass_kernel_spmd(nc, in_maps, core_ids=list(range(NCORES)))
    return np.concatenate([np.asarray(r["y"]) for r in res.results], axis=0).astype(np.float32)
```
